# Optimizing a Trainium2 kernel written in Bass

```python
import jax, jax.numpy as jnp
from jax import lax
import numpy as np

D_MODEL = 1024
BATCH = 2
SEQ = 16384
DEPTH = 4

N_MIXERS = 2
N_POOL_LAYERS = (DEPTH + 1) // 2
N_MLA_LAYERS = DEPTH // 2
POOL_WINDOWS = (2, 4, 8, 16)
N_POOL_GROUPS = len(POOL_WINDOWS)
POOL_GROUP = D_MODEL // N_POOL_GROUPS
N_HEADS = D_MODEL // 128
QK_NOPE = 128
QK_ROPE = 64
QK_HEAD = QK_NOPE + QK_ROPE
V_HEAD = 128
Q_LORA = 3 * D_MODEL // 4
KV_LORA = D_MODEL // 4
ROPE_THETA = 10000.0
Q_BLOCK = 128
D_FF = 2816
FFN_HALF = 0.5
EPS = 1e-6

kernel_name = "hybrid_pool_mla_macaron_trunk"


def rmsnorm(x, gain):
    x32 = x.astype(jnp.float32)
    y = x32 * lax.rsqrt(jnp.mean(x32 * x32, axis=-1, keepdims=True) + EPS)
    return (y * gain.astype(jnp.float32)).astype(x.dtype)


def swiglu(h, w_gate, w_up, w_down):
    return (jax.nn.silu(h @ w_gate) * (h @ w_up)) @ w_down


def pool_mixer(h, w, scale):
    B, S, D = h.shape
    u = h.astype(jnp.float32).reshape(B, S, N_POOL_GROUPS, POOL_GROUP)
    cs = jnp.pad(jnp.cumsum(u, axis=1), ((0, 0), (1, 0), (0, 0), (0, 0)))
    sums = []
    for g, w_len in enumerate(POOL_WINDOWS):
        c = cs[:, :, g]
        lower = jnp.pad(c[:, :S + 1 - w_len], ((0, 0), (w_len - 1, 0), (0, 0)))
        sums.append(c[:, 1:] - lower)
    window_sum = jnp.stack(sums, axis=2)
    t = jnp.arange(S)
    count = jnp.minimum(t[:, None] + 1, jnp.array(POOL_WINDOWS, jnp.int32)[None, :])
    pooled = window_sum / count.astype(jnp.float32)[None, :, :, None] - u
    y = jnp.einsum('bsgc,gcd->bsgd', pooled.astype(h.dtype), w).reshape(B, S, D)
    return y * scale


def rope_tail(x, cos, sin):
    x_nope, x_pe = jnp.split(x, [QK_NOPE], axis=-1)
    x1, x2 = jnp.split(x_pe, 2, axis=-1)
    c = cos[:, :, None, :].astype(x.dtype)
    s = sin[:, :, None, :].astype(x.dtype)
    return jnp.concatenate([x_nope, x1 * c - x2 * s, x2 * c + x1 * s], axis=-1)


def causal_attention(q, k, v):
    B, S, H, Dq = q.shape
    nb = S // Q_BLOCK
    qb = q.reshape(B, nb, Q_BLOCK, H, Dq).transpose(1, 0, 3, 2, 4)
    kt = k.transpose(0, 2, 1, 3)
    vt = v.transpose(0, 2, 1, 3)
    kpos = jnp.arange(S)
    scale = QK_HEAD ** -0.5

    def one_block(args):
        q_blk, blk = args
        s = jnp.einsum('bhqd,bhkd->bhqk', q_blk, kt,
                       preferred_element_type=jnp.float32) * scale
        qpos = blk * Q_BLOCK + jnp.arange(Q_BLOCK)
        s = jnp.where(kpos[None, :] <= qpos[:, None], s, -jnp.inf)
        p = jax.nn.softmax(s, axis=-1).astype(vt.dtype)
        return jnp.einsum('bhqk,bhkv->bhqv', p, vt)

    out = lax.map(one_block, (qb, jnp.arange(nb)))
    return out.transpose(1, 0, 3, 2, 4).reshape(B, S, H, V_HEAD)


def mla_mixer(h, cos, sin, w_in, q_norm, w_q_up, kv_norm, w_kv_up,
              q_head_norm, k_head_norm, w_out):
    B, S, _ = h.shape
    lat = h @ w_in
    cq, ckv, k_pe = jnp.split(lat, [Q_LORA, Q_LORA + KV_LORA], axis=-1)
    q = (rmsnorm(cq, q_norm) @ w_q_up).reshape(B, S, N_HEADS, QK_HEAD)
    kv = (rmsnorm(ckv, kv_norm) @ w_kv_up).reshape(B, S, N_HEADS, QK_NOPE + V_HEAD)
    k_nope, v = jnp.split(kv, [QK_NOPE], axis=-1)
    k = jnp.concatenate(
        [k_nope, jnp.broadcast_to(k_pe[:, :, None, :], (B, S, N_HEADS, QK_ROPE))], axis=-1)
    q = rope_tail(rmsnorm(q, q_head_norm), cos, sin)
    k = rope_tail(rmsnorm(k, k_head_norm), cos, sin)
    o = causal_attention(q, k, v)
    return o.reshape(B, S, N_HEADS * V_HEAD) @ w_out


def setup_inputs(seed: int = 0) -> dict:
    key = jax.random.key(seed)
    ks = jax.random.split(key, 24)
    f32 = jnp.float32

    def dense(k, shape, fan_in):
        return jax.random.normal(k, shape, f32) * fan_in ** -0.5

    def gain(k, shape):
        return 1.0 + 0.05 * jax.random.normal(k, shape, f32)

    x = jax.random.normal(ks[0], (BATCH, SEQ, D_MODEL), f32)
    offsets = jax.random.randint(ks[1], (BATCH, 1), 0, 4096, dtype=jnp.int32)
    positions = (jnp.arange(SEQ, dtype=jnp.int32)[None, :] + offsets).astype(jnp.int32)
    return {
        "x": x,
        "positions": positions,
        "ffn1_norm": gain(ks[2], (DEPTH, D_MODEL)),
        "ffn1_w_gate": dense(ks[3], (DEPTH, D_MODEL, D_FF), D_MODEL),
        "ffn1_w_up": dense(ks[4], (DEPTH, D_MODEL, D_FF), D_MODEL),
        "ffn1_w_down": dense(ks[5], (DEPTH, D_FF, D_MODEL), D_FF),
        "mix_norm": gain(ks[6], (DEPTH, D_MODEL)),
        "pool_w": dense(ks[7], (N_POOL_LAYERS, N_POOL_GROUPS, POOL_GROUP, POOL_GROUP), POOL_GROUP),
        "pool_scale": gain(ks[8], (N_POOL_LAYERS, D_MODEL)),
        "mla_w_in": dense(ks[9], (N_MLA_LAYERS, D_MODEL, Q_LORA + KV_LORA + QK_ROPE), D_MODEL),
        "mla_q_norm": gain(ks[10], (N_MLA_LAYERS, Q_LORA)),
        "mla_w_q_up": dense(ks[11], (N_MLA_LAYERS, Q_LORA, N_HEADS * QK_HEAD), Q_LORA),
        "mla_kv_norm": gain(ks[12], (N_MLA_LAYERS, KV_LORA)),
        "mla_w_kv_up": dense(ks[13], (N_MLA_LAYERS, KV_LORA, N_HEADS * (QK_NOPE + V_HEAD)), KV_LORA),
        "mla_q_head_norm": gain(ks[14], (N_MLA_LAYERS, QK_HEAD)),
        "mla_k_head_norm": gain(ks[15], (N_MLA_LAYERS, QK_HEAD)),
        "mla_w_out": dense(ks[16], (N_MLA_LAYERS, N_HEADS * V_HEAD, D_MODEL), N_HEADS * V_HEAD),
        "ffn2_norm": gain(ks[17], (DEPTH, D_MODEL)),
        "ffn2_w_gate": dense(ks[18], (DEPTH, D_MODEL, D_FF), D_MODEL),
        "ffn2_w_up": dense(ks[19], (DEPTH, D_MODEL, D_FF), D_MODEL),
        "ffn2_w_down": dense(ks[20], (DEPTH, D_FF, D_MODEL), D_FF),
    }


def reference(x, positions, ffn1_norm, ffn1_w_gate, ffn1_w_up, ffn1_w_down, mix_norm,
              pool_w, pool_scale, mla_w_in, mla_q_norm, mla_w_q_up, mla_kv_norm,
              mla_w_kv_up, mla_q_head_norm, mla_k_head_norm, mla_w_out,
              ffn2_norm, ffn2_w_gate, ffn2_w_up, ffn2_w_down):
    inv_freq = 1.0 / (ROPE_THETA ** (jnp.arange(0, QK_ROPE, 2, dtype=jnp.float32) / QK_ROPE))
    ang = positions.astype(jnp.float32)[..., None] * inv_freq
    cos, sin = jnp.cos(ang), jnp.sin(ang)

    for i in range(DEPTH):
        h = rmsnorm(x, ffn1_norm[i])
        x = x + FFN_HALF * swiglu(h, ffn1_w_gate[i], ffn1_w_up[i], ffn1_w_down[i])
        h = rmsnorm(x, mix_norm[i])
        j = i // N_MIXERS
        if i % N_MIXERS == 0:
            x = x + pool_mixer(h, pool_w[j], pool_scale[j])
        else:
            x = x + mla_mixer(h, cos, sin, mla_w_in[j], mla_q_norm[j], mla_w_q_up[j],
                              mla_kv_norm[j], mla_w_kv_up[j], mla_q_head_norm[j],
                              mla_k_head_norm[j], mla_w_out[j])
        h = rmsnorm(x, ffn2_norm[i])
        x = x + FFN_HALF * swiglu(h, ffn2_w_gate[i], ffn2_w_up[i], ffn2_w_down[i])
    return x
```

```python
import math
from contextlib import ExitStack

import numpy as np
import ml_dtypes
import concourse.bass as bass
import concourse.mybir as mybir
from concourse.bass_utils import run_bass_kernel_spmd

F32 = mybir.dt.float32
BF16 = mybir.dt.bfloat16
I32 = mybir.dt.int32
ALU = mybir.AluOpType
AF = mybir.ActivationFunctionType

NCORES = 8
D = 1024
DFF = 2816
KC = 8
FC = 22
S = 16384
TOK = 4096
TT = 1024
NT = TOK // TT
HALF = 512
EPS = 1e-6
QL = 768
KVL = 256
LATR = 1088
NQT = S // HALF
SCALE = 192.0 ** -0.5

GP = {}
_o = 0
for _l in range(4):
    for _n in ("f1n", "mixn", "f2n"):
        GP[(_n, _l)] = _o
        _o += 8
for _j in range(2):
    GP[("pscale", _j)] = _o
    _o += 8
for _j in range(2):
    GP[("qn", _j)] = _o
    _o += 6
    GP[("kvn", _j)] = _o
    _o += 2
    GP[("qhn_n", _j)] = _o
    _o += 1
    GP[("qhn_p", _j)] = _o
    _o += 1
    GP[("khn_n", _j)] = _o
    _o += 1
    GP[("khn_p", _j)] = _o
    _o += 1
GP["invf"] = _o
_o += 1
NG = _o

MAGIC = 12582912.0
TWO_PI = 2.0 * math.pi
C1 = 6.28125
C2 = float(np.float32(0.0019350052))
C3 = TWO_PI - C1 - C2
PI_SAFE = 3.1415925


class Buf:
    __slots__ = ("name", "w", "r", "dsem")

    def __init__(self, name=""):
        self.name = name
        self.w = None
        self.r = {}
        self.dsem = None


class Prog:
    def __init__(self, nc):
        self.nc = nc
        self.es = ExitStack()
        self.eng = {"pe": nc.tensor, "act": nc.scalar, "dve": nc.vector, "pool": nc.gpsimd, "sp": nc.sync}
        self.semobj = {}
        self.cnt = {}
        for k in ("pe", "act", "dve", "pool"):
            self.semobj[k] = self.es.enter_context(nc.semaphore("prog_" + k))
            self.cnt[k] = 0
        self.seen = {k: {} for k in self.eng}
        self.dval = {}
        self.nd = 0
        self.free_dsems = []

    def new_dsem(self):
        if self.free_dsems:
            return self.free_dsems.pop()
        sid = "d%d" % self.nd
        self.nd += 1
        self.semobj[sid] = self.es.enter_context(self.nc.semaphore("dma_%s" % sid))
        self.dval[sid] = 0
        return sid

    def release(self, bufs):
        for b in bufs:
            if b.dsem is not None:
                self.free_dsems.append(b.dsem)
                b.dsem = None

    def wait(self, e, tok):
        if tok is None:
            return
        sid, val = tok
        if e == "pe" and sid == "pe":
            return
        if self.seen[e].get(sid, 0) >= val:
            return
        self.eng[e].wait_ge(self.semobj[sid], val)
        self.seen[e][sid] = val

    def _deps(self, e, reads, writes):
        for b in reads:
            self.wait(e, b.w)
        for b in writes:
            self.wait(e, b.w)
            for t in b.r.values():
                self.wait(e, t)

    def _commit(self, tok, key, reads, writes):
        for b in reads:
            b.r[key] = tok
        for b in writes:
            b.w = tok
            b.r = {}

    def op(self, e, fn, reads=(), writes=()):
        self._deps(e, reads, writes)
        ins = fn(self.eng[e])
        self.cnt[e] += 1
        ins.then_inc(self.semobj[e], 1)
        tok = (e, self.cnt[e])
        self._commit(tok, e, reads, writes)
        return tok

    def mm(self, out_ap, pairs, reads=(), writes=(), start=True, stop=True, signal=True):
        e = "pe"
        self._deps(e, reads, writes)
        n = len(pairs)
        ins = None
        for i, (l, r) in enumerate(pairs):
            ins = self.nc.tensor.matmul(out_ap, lhsT=l, rhs=r, start=(start and i == 0), stop=(stop and i == n - 1))
        if signal:
            self.cnt[e] += 1
            ins.then_inc(self.semobj[e], 1)
            tok = (e, self.cnt[e])
            self._commit(tok, e, reads, writes)
            return tok
        return None

    def dma(self, q, out, in_, reads=(), writes=(), **kw):
        self._deps(q, reads, writes)
        w0 = writes[0]
        if w0.dsem is None:
            w0.dsem = self.new_dsem()
        sid = w0.dsem
        ins = self.eng[q].dma_start(out=out, in_=in_, **kw)
        self.dval[sid] += 16
        ins.then_inc(self.semobj[sid], 16)
        tok = (sid, self.dval[sid])
        self._commit(tok, sid, reads, writes)
        return tok

    def allgather(self, in_ap, out_ap, groups, reads, writes, fam="cc"):
        e = "pool"
        self._deps(e, reads, writes)
        sid = "cc_" + fam
        if sid not in self.semobj:
            self.semobj[sid] = self.es.enter_context(self.nc.semaphore("ccsem_" + fam))
            self.dval[sid] = 0
        ins = self.nc.gpsimd.collective_compute("AllGather", ALU.bypass, replica_groups=groups,
                                                ins=[in_ap], outs=[out_ap])
        self.dval[sid] += 1
        ins.then_inc(self.semobj[sid])
        tok = (sid, self.dval[sid])
        self._commit(tok, sid, reads, writes)
        return tok

    def barrier(self, scratch_ap):
        e = "pool"
        for k in ("pe", "act", "dve"):
            if self.cnt[k]:
                self.wait(e, (k, self.cnt[k]))
        for sid, v in self.dval.items():
            if v:
                self.wait(e, (sid, v))
        ins = self.nc.gpsimd.memset(scratch_ap, 0.0)
        self.cnt[e] += 1
        ins.then_inc(self.semobj[e], 1)
        tok = (e, self.cnt[e])
        for k in ("pe", "act", "dve", "sp"):
            self.wait(k, tok)
        return tok


def build_program(upto=99, dbg=()):
    nc = bass.Bass("TRN2", target_bir_lowering=False)
    P = Prog(nc)
    es = P.es

    def din(name, shape, dt=F32):
        return nc.dram_tensor(name, list(shape), dt, kind="ExternalInput")

    def dscr(name, shape, dt):
        return nc.dram_tensor(name, list(shape), dt)

    xT_in = din("xT", [D, TOK])
    pos_in = din("pos", [1, S], I32)
    w_in = {}
    FAM = {}
    for f in ("f1", "f2"):
        FAM[f + "g"] = (4 * D, DFF)
        FAM[f + "u"] = (4 * D, DFF)
        FAM[f + "d"] = (4 * DFF, D)
    FAM["poolw"] = (2 * 1024, 256)
    FAM["win"] = (2 * D, LATR)
    FAM["wout"] = (2 * D, D)
    for fam, (rr, cc) in FAM.items():
        w_in[fam] = din(fam, [rr // NCORES, cc])
    wq_in = din("wqm", [2 * QL, 384])
    wkv_in = din("wkvm", [2 * KVL, 512])
    gpack_in = din("gpack", [128, NG])
    hn_in = din("hn", [1, 4 * 192])
    masks_in = din("masks", [128, 4 * HALF], BF16)
    rotm_in = din("rotm", [128, 128])
    invcnt_in = din("invcnt", [128, 137])
    outT = nc.dram_tensor("outT", [D, TOK], F32, kind="ExternalOutput")

    class APW:
        def __init__(self, ap_):
            self._ap = ap_

        def ap(self):
            return self._ap

    wb = {}
    wbuf = {}
    wsh = {}
    wshb = {}
    wall = {}
    wallb = {}
    for fam, (rr, cc) in FAM.items():
        wsh[fam] = dscr("wsh_" + fam, [rr // NCORES, cc], BF16)
        wshb[fam] = Buf()
        wall[fam] = dscr("wall_" + fam, [rr, cc], BF16)
        wallb[fam] = Buf()
        nl = 4 if fam[0] == "f" else 2
        per = rr // nl
        for l in range(nl):
            wb[(fam, l)] = APW(wall[fam].ap()[l * per:(l + 1) * per, :])
            wbuf[(fam, l)] = wallb[fam]
    for j in range(2):
        wb[("wq", j)] = dscr("wb_wq%d" % j, [QL, 384], BF16)
        wb[("wkv", j)] = dscr("wb_wkv%d" % j, [KVL, 512], BF16)
        for n in ("wq", "wkv"):
            wbuf[(n, j)] = Buf()
    groups8 = [list(range(NCORES))]
    XP = dscr("XP", [D, TOK], F32)
    XPb = [Buf() for _ in range(NT)]
    NQ4 = NT * 4
    LATS = [dscr("LATS%d" % i, [LATR, 256], BF16) for i in range(NQ4)]
    LATSb = [Buf() for _ in range(NQ4 // 2)]
    LATALL = [dscr("LATALL%d" % i, [4 * LATR, 256], BF16) for i in range(NQ4)]
    LATALLb = [Buf() for _ in range(NQ4 // 2)]
    OSEND = [[dscr("OSEND%d_%d" % (hh, g), [128, TOK], BF16) for g in range(4)] for hh in range(2)]
    OSENDb = [[Buf() for g in range(4)] for hh in range(2)]
    OALL = [[dscr("OALL%d_%d" % (hh, g), [512, TOK], BF16) for g in range(4)] for hh in range(2)]
    OALLb = [[Buf() for g in range(4)] for hh in range(2)]
    HSEND = dscr("HSEND", [D, 16], F32)
    HSENDb = Buf()
    HALLZ = dscr("HALLZ", [4 * D, 16], F32)
    HALLZb = Buf()
    ROPEC = dscr("ROPEC", [64, S], F32)
    ROPES = dscr("ROPES", [64, S], F32)
    ROPEb = Buf()
    groups4 = [[0, 1, 2, 3], [4, 5, 6, 7]]

    uniq = [0]

    def uname(name):
        uniq[0] += 1
        return "sb%d_%s" % (uniq[0], name)

    def sb(name, shape, dt):
        return es.enter_context(nc.sbuf_tensor(uname(name), list(shape), dt))

    gpack = sb("gpack", [128, NG], F32)
    gpackb = Buf()
    ones = sb("ones", [128, 128], BF16)
    onesb = Buf()
    eps_t = sb("eps_t", [128, 1], F32)
    rotm = sb("rotm", [128, 128], F32)
    rotmb = Buf()
    masks = sb("masks", [128, 4 * HALF], BF16)
    masksb = Buf()
    negc = sb("negc", [128, 2], F32)
    negcb = Buf()
    barscr = sb("barscr", [128, 1], F32)
    pc = sb("pc", [128, 137], F32)
    pcb = Buf()
    ps = [es.enter_context(nc.psum_tensor("ps%d" % i, [128, HALF], F32)) for i in range(8)]
    psb = [Buf("ps%d" % i) for i in range(8)]

    def gcol(key, c=0):
        o = GP[key] + c
        return gpack[:, o:o + 1]

    P.dma("sp", gpack[:], gpack_in.ap(), writes=[gpackb])
    P.dma("sp", rotm[:], rotm_in.ap(), writes=[rotmb])
    P.dma("sp", masks[:], masks_in.ap(), writes=[masksb])
    P.dma("sp", pc[:], invcnt_in.ap(), writes=[pcb])
    P.op("dve", lambda v: v.memset(ones[:], 1.0), writes=[onesb])
    P.op("dve", lambda v: v.memset(eps_t[:], EPS), writes=[onesb])

    def cast_dma(dst_ap, src_ap, dstbuf):
        r, c = dst_ap.shape
        b = 1
        for cand in (11, 8, 4, 2):
            if r % cand == 0 and (r // cand) >= 16:
                b = cand
                break
        P.dma("pool", dst_ap.rearrange("(a b) c -> a (b c)", b=b), src_ap.rearrange("(a b) c -> a (b c)", b=b),
              writes=[dstbuf], max_dma_last_dim=4096)

    def cast_family(fam):
        cast_dma(wsh[fam].ap(), w_in[fam].ap(), wshb[fam])
        P.allgather(wsh[fam].ap(), wall[fam].ap(), groups8, reads=[wshb[fam]], writes=[wallb[fam]], fam="w")

    def cast_small():
        for j in range(2):
            cast_dma(wb[("wq", j)].ap(), wq_in.ap()[j * QL:(j + 1) * QL, :], wbuf[("wq", j)])
            cast_dma(wb[("wkv", j)].ap(), wkv_in.ap()[j * KVL:(j + 1) * KVL, :], wbuf[("wkv", j)])


    class TokBufs:
        pass

    def alloc_tok(st):
        t = TokBufs()

        def a(name, shape, dt):
            return st.enter_context(nc.sbuf_tensor(uname(name), list(shape), dt))

        t.xT = a("xT", [128, KC, TT], F32)
        t.xTb = [Buf("xT%d" % c) for c in range(KC)]
        t.hT = a("hT", [128, KC, TT], BF16)
        t.hTb = [Buf("hT0"), Buf("hT1")]
        t.big = a("big", [128, FC * TT], BF16)
        t.bigb = [Buf("big0"), Buf("big1")]
        t.wg = [a("wg%d" % i, [128, KC, 256], BF16) for i in range(2)]
        t.wu = [a("wu%d" % i, [128, KC, 256], BF16) for i in range(2)]
        t.wgb = [Buf() for _ in range(2)]
        t.wub = [Buf() for _ in range(2)]
        t.wd = [a("wd%d" % i, [128, FC, 256], BF16) for i in range(2)]
        t.wdb = [Buf() for _ in range(2)]
        t.wst = a("wst", [128, KC * 1152], BF16)
        t.wstb = Buf()
        t.rstd = a("rstd", [128, TT], F32)
        t.rstdb = Buf()
        t.sqv = a("sqv", [128, TT], F32)
        t.sqvb = Buf()
        t.sq = [a("sq%d" % i, [128, HALF], BF16) for i in range(2)]
        t.sqb = [Buf() for _ in range(2)]
        t.sil = [a("sil%d" % i, [128, HALF], F32) for i in range(2)]
        t.silb = [Buf() for _ in range(2)]
        t.halo = a("halo", [128, KC, 16], F32)
        t.halob = Buf()
        t.pa = a("pa", [128, TT + 16], F32)
        t.pab = Buf()
        t.pb = a("pb", [128, TT + 16], F32)
        t.pbb = Buf()
        t.candb = [[Buf(), Buf()] for _ in range(4)]
        t.hal4 = a("hal4", [128, 32, 16], F32)
        t.hal4b = Buf()
        t.allb = (t.xTb + t.hTb + t.bigb + t.wgb + t.wub + t.wdb +
                  [t.wstb, t.rstdb, t.sqvb, t.halob, t.pab, t.pbb, t.hal4b] + [b_ for p_ in t.candb for b_ in p_] + t.sqb + t.silb)
        return t

    def cols(tile):
        return slice(tile * TT, (tile + 1) * TT)

    def load_x(t, src, srcbufs, tile):
        P.dma("sp", t.xT[:, :, :], src.ap()[:, cols(tile)].rearrange("(c p) n -> p c n", p=128),
              reads=srcbufs, writes=t.xTb)

    def store_x(t, dst, dstbufs, tile, q="sp"):
        P.dma(q, dst.ap()[:, cols(tile)].rearrange("(c p) n -> p c n", p=128), t.xT[:, :, :],
              reads=t.xTb, writes=dstbufs)

    sqi = [0]

    def rms_rstd(t, src_fn, nch, n_feat, c0, c1, srcbufs, psbank=6):
        n = c1 - c0
        for c in range(nch):
            i = sqi[0] % 2
            sqi[0] += 1
            P.op("act", lambda a, c=c, i=i: a.activation(out=t.sq[i][:, 0:n], in_=src_fn(c), func=AF.Square),
                 reads=srcbufs, writes=[t.sqb[i]])
            P.mm(ps[psbank][:, 0:n], [(ones[:], t.sq[i][:, 0:n])], reads=[t.sqb[i], onesb], writes=[psb[psbank]],
                 start=(c == 0), stop=(c == nch - 1))
        P.op("act", lambda a: a.activation(out=t.sqv[:, c0:c1], in_=ps[psbank][:, 0:n], func=AF.Sqrt,
                                           bias=eps_t[:, 0:1], scale=1.0 / n_feat),
             reads=[psb[psbank]], writes=[t.sqvb])
        P.op("dve", lambda v: v.reciprocal(out=t.rstd[:, c0:c1], in_=t.sqv[:, c0:c1]),
             reads=[t.sqvb], writes=[t.rstdb])

    def norm_to_hT(t, gkey):
        for h in range(2):
            c0, c1 = h * HALF, (h + 1) * HALF
            rms_rstd(t, lambda c: t.xT[:, c, c0:c1], KC, D, c0, c1, t.xTb)
            for c in range(KC):
                P.op("dve", lambda v, c=c: v.scalar_tensor_tensor(
                    out=t.hT[:, c, c0:c1], in0=t.xT[:, c, c0:c1], scalar=gcol(gkey, c),
                    in1=t.rstd[:, c0:c1], op0=ALU.mult, op1=ALU.mult),
                    reads=[t.xTb[c], t.rstdb, gpackb], writes=[t.hTb[h]])

    gui = [0]
    wdi = [0]

    def ffn(t, f, l):
        gkey = (f + "n", l)
        WG, WU, WD = wb[(f + "g", l)], wb[(f + "u", l)], wb[(f + "d", l)]
        WGb, WUb, WDb = wbuf[(f + "g", l)], wbuf[(f + "u", l)], wbuf[(f + "d", l)]
        aT = t.big

        def aslice(m, c0, c1):
            return aT[:, m * TT + c0:m * TT + c1]

        npiece = FC // 2

        def load_gu(j):
            s = gui[0] % 2
            gui[0] += 1
            P.dma("sp", t.wg[s][:, :, :], WG.ap()[:, j * 256:(j + 1) * 256].rearrange("(k p) c -> p k c", p=128),
                  reads=[WGb], writes=[t.wgb[s]])
            P.dma("sp", t.wu[s][:, :, :], WU.ap()[:, j * 256:(j + 1) * 256].rearrange("(k p) c -> p k c", p=128),
                  reads=[WUb], writes=[t.wub[s]])
            return s

        def load_d(j):
            s = wdi[0] % 2
            wdi[0] += 1
            P.dma("sp", t.wd[s][:, :, :], WD.ap()[:, j * 256:(j + 1) * 256].rearrange("(f p) c -> p f c", p=128),
                  reads=[WDb], writes=[t.wdb[s]])
            return s

        slots = {0: load_gu(0)}
        norm_to_hT(t, gkey)
        pi = 0
        for j in range(npiece):
            if j + 1 < npiece:
                slots[j + 1] = load_gu(j + 1)
            s = slots[j]
            for mm_ in range(2):
                m = 2 * j + mm_
                for h in range(2):
                    c0, c1 = h * HALF, (h + 1) * HALF
                    bg, bu = (pi % 2) * 2, (pi % 2) * 2 + 1
                    pi += 1
                    P.mm(ps[bg][:], [(t.wg[s][:, k, mm_ * 128:(mm_ + 1) * 128], t.hT[:, k, c0:c1]) for k in range(KC)],
                         reads=[t.wgb[s], t.hTb[h]], writes=[psb[bg]])
                    P.mm(ps[bu][:], [(t.wu[s][:, k, mm_ * 128:(mm_ + 1) * 128], t.hT[:, k, c0:c1]) for k in range(KC)],
                         reads=[t.wub[s], t.hTb[h]], writes=[psb[bu]])
                    si = pi % 2
                    P.op("act", lambda a, bg=bg, si=si: a.activation(out=t.sil[si][:], in_=ps[bg][:], func=AF.Silu),
                         reads=[psb[bg]], writes=[t.silb[si]])
                    P.op("dve", lambda v, bu=bu, si=si, m=m, c0=c0, c1=c1: v.tensor_tensor(
                        out=aslice(m, c0, c1), in0=t.sil[si][:], in1=ps[bu][:], op=ALU.mult),
                        reads=[t.silb[si], psb[bu]], writes=[t.bigb[h]])
        dslots = {0: load_d(0)}
        di = 0
        for j in range(4):
            if j + 1 < 4:
                dslots[j + 1] = load_d(j + 1)
            s = dslots[j]
            for dd in range(2):
                dc = 2 * j + dd
                for h in range(2):
                    c0, c1 = h * HALF, (h + 1) * HALF
                    b = 4 + (di % 2)
                    di += 1
                    P.mm(ps[b][:], [(t.wd[s][:, fch, dd * 128:(dd + 1) * 128], aslice(fch, c0, c1)) for fch in range(FC)],
                         reads=[t.wdb[s], t.bigb[h]], writes=[psb[b]])
                    P.op("dve", lambda v, b=b, dc=dc, c0=c0, c1=c1: v.scalar_tensor_tensor(
                        out=t.xT[:, dc, c0:c1], in0=ps[b][:], scalar=0.5, in1=t.xT[:, dc, c0:c1],
                        op0=ALU.mult, op1=ALU.add),
                        reads=[psb[b], t.xTb[dc]], writes=[t.xTb[dc]])

    def halo_tail_to_send(t, l):
        c0, c1 = TT - 16, TT
        rms_rstd(t, lambda c: t.xT[:, c, c0:c1], KC, D, c0, c1, t.xTb)
        for c in range(KC):
            P.op("dve", lambda v, c=c: v.scalar_tensor_tensor(
                out=t.halo[:, c, :], in0=t.xT[:, c, c0:c1], scalar=gcol(("mixn", l), c),
                in1=t.rstd[:, c0:c1], op0=ALU.mult, op1=ALU.mult),
                reads=[t.xTb[c], t.rstdb, gpackb], writes=[t.halob])
        P.dma("sp", HSEND.ap().rearrange("(c p) n -> p c n", p=128), t.halo[:, :, :],
              reads=[t.halob], writes=[HSENDb])

    def halo_exchange():
        P.allgather(HSEND.ap(), HALLZ.ap(), groups4, reads=[HSENDb], writes=[HALLZb], fam="h")

    def pool_mixer(t, l, tile):
        j = l // 2
        hm = t.big[:, :].bitcast(F32)
        W = TT + 16

        def hmv(c, a, b):
            return hm[:, c * W + a:c * W + b]

        P.dma("sp", t.wst[:, 0:8 * 256].rearrange("p (a c) -> p a c", c=256),
              wb[("poolw", j)].ap().rearrange("(a p) c -> p a c", p=128),
              reads=[wbuf[("poolw", j)]], writes=[t.wstb])
        if tile == 0:
            P.dma("sp", t.hal4[:, :, :], HALLZ.ap().rearrange("(a p) n -> p a n", p=128),
                  reads=[HALLZb], writes=[t.hal4b])
            for sl_ in range(4):
                if sl_ == 0:
                    P.op("dve", lambda v: v.tensor_scalar(out=t.halo[:, :, :], in0=t.hal4[:, 0:8, :],
                                                          scalar1=pc[:, 133:134], scalar2=None, op0=ALU.mult),
                         reads=[t.hal4b, pcb], writes=[t.halob])
                else:
                    P.op("dve", lambda v, sl_=sl_: v.scalar_tensor_tensor(
                        out=t.halo[:, :, :], in0=t.hal4[:, sl_ * 8:(sl_ + 1) * 8, :], scalar=pc[:, 133 + sl_:134 + sl_],
                        in1=t.halo[:, :, :], op0=ALU.mult, op1=ALU.add),
                        reads=[t.hal4b, pcb, t.halob], writes=[t.halob])
        for c in range(KC):
            P.op("dve", lambda v, c=c: v.tensor_copy(out=hmv(c, 0, 16), in_=t.halo[:, c, :]),
                 reads=[t.halob], writes=t.bigb)
        for h in range(2):
            c0, c1 = h * HALF, (h + 1) * HALF
            rms_rstd(t, lambda c: t.xT[:, c, c0:c1], KC, D, c0, c1, t.xTb)
            for c in range(KC):
                P.op("dve", lambda v, c=c: v.scalar_tensor_tensor(
                    out=hmv(c, 16 + c0, 16 + c1), in0=t.xT[:, c, c0:c1], scalar=gcol(("mixn", l), c),
                    in1=t.rstd[:, c0:c1], op0=ALU.mult, op1=ALU.mult),
                    reads=[t.xTb[c], t.rstdb, gpackb], writes=t.bigb)
        for c in range(KC):
            P.op("dve", lambda v, c=c: v.tensor_copy(out=t.halo[:, c, :], in_=hmv(c, TT, TT + 16)),
                 reads=t.bigb, writes=[t.halob])
        for c in range(KC):
            g = c // 2
            w = 2 << g
            u = lambda a, b, c=c: hmv(c, a, b)
            cur = u
            bufs = [(t.pa, t.pab), (t.pb, t.pbb)]
            sh = 1
            lvl = 0
            off = 0
            curb = t.bigb
            while sh < w:
                dstt, dstb = bufs[lvl % 2]
                off2 = off + sh
                P.op("dve", lambda v, cur=cur, dstt=dstt, off2=off2, sh=sh: v.tensor_tensor(
                    out=dstt[:, off2:W], in0=cur(off2, W), in1=cur(off2 - sh, W - sh), op=ALU.add),
                    reads=list(curb), writes=[dstb])
                cur = (lambda a, b, dstt=dstt: dstt[:, a:b])
                curb = [dstb]
                off = off2
                sh *= 2
                lvl += 1
            for h in range(2):
                c0, c1 = h * HALF, (h + 1) * HALF
                P.op("dve", lambda v, cur=cur, c=c, c0=c0, c1=c1, w=w: v.scalar_tensor_tensor(
                    out=t.hT[:, c, c0:c1], in0=cur(16 + c0, 16 + c1), scalar=1.0 / w, in1=hmv(c, 16 + c0, 16 + c1),
                    op0=ALU.mult, op1=ALU.subtract),
                    reads=list(curb) + t.bigb, writes=[t.hTb[h]])
            if tile == 0:
                P.op("dve", lambda v, cur=cur, c=c: v.tensor_tensor(
                    out=t.sqv[:, 0:16], in0=cur(16, 32), in1=pc[:, c * 16:(c + 1) * 16], op=ALU.mult),
                    reads=list(curb) + [pcb], writes=[t.sqvb])
                P.op("dve", lambda v, c=c: v.tensor_tensor(
                    out=t.hT[:, c, 0:16], in0=t.sqv[:, 0:16], in1=hmv(c, 16, 32), op=ALU.subtract),
                    reads=[t.sqvb] + t.bigb, writes=[t.hTb[0]])
        bi = 0
        for c in range(KC):
            g = c // 2
            cc = c % 2
            for h in range(2):
                c0, c1 = h * HALF, (h + 1) * HALF
                b = 4 + (bi % 2)
                bi += 1
                P.mm(ps[b][:], [(t.wst[:, (g * 2 + k) * 256 + cc * 128:(g * 2 + k) * 256 + (cc + 1) * 128],
                                 t.hT[:, g * 2 + k, c0:c1]) for k in range(2)],
                     reads=[t.wstb, t.hTb[h]], writes=[psb[b]])
                P.op("dve", lambda v, b=b, c=c, c0=c0, c1=c1: v.scalar_tensor_tensor(
                    out=t.xT[:, c, c0:c1], in0=ps[b][:], scalar=gcol(("pscale", j), c), in1=t.xT[:, c, c0:c1],
                    op0=ALU.mult, op1=ALU.add),
                    reads=[psb[b], t.xTb[c], gpackb], writes=[t.xTb[c]])

    def mla_latents(t, l, tile):
        j = l // 2
        WIN = wb[("win", j)]
        wv = t.wst[:, 0:KC * 1152].rearrange("p (k c) -> p k c", c=1152)
        P.op("dve", lambda v: v.memset(t.wst[:, :], 0.0), writes=[t.wstb])
        P.dma("sp", wv[:, :, 0:LATR], WIN.ap().rearrange("(k p) c -> p k c", p=128),
              reads=[wbuf[("win", j)]], writes=[t.wstb])
        norm_to_hT(t, ("mixn", l))
        latf = t.big[:, :].bitcast(F32)
        latb = t.big
        BO = 9216
        for h in range(2):
            c0, c1 = h * HALF, (h + 1) * HALF
            for oc in range(9):
                b = 4 + (oc % 2)
                P.mm(ps[b][:], [(wv[:, k, oc * 128:(oc + 1) * 128], t.hT[:, k, c0:c1]) for k in range(KC)],
                     reads=[t.wstb, t.hTb[h]], writes=[psb[b]])
                P.op("act", lambda a, b=b, oc=oc: a.activation(out=latf[:, oc * HALF:(oc + 1) * HALF], in_=ps[b][:],
                                                               func=AF.Copy),
                     reads=[psb[b]], writes=t.bigb)
            rms_rstd(t, lambda c: latf[:, c * HALF:(c + 1) * HALF], 6, QL, 0, HALF, t.bigb)
            for c in range(6):
                P.op("dve", lambda v, c=c: v.scalar_tensor_tensor(
                    out=latb[:, BO + c * HALF:BO + (c + 1) * HALF], in0=latf[:, c * HALF:(c + 1) * HALF],
                    scalar=gcol(("qn", j), c), in1=t.rstd[:, 0:HALF], op0=ALU.mult, op1=ALU.mult),
                    reads=t.bigb + [t.rstdb, gpackb], writes=t.bigb)
            rms_rstd(t, lambda c: latf[:, (6 + c) * HALF:(7 + c) * HALF], 2, KVL, 0, HALF, t.bigb)
            for c in range(2):
                P.op("dve", lambda v, c=c: v.scalar_tensor_tensor(
                    out=latb[:, BO + (6 + c) * HALF:BO + (7 + c) * HALF], in0=latf[:, (6 + c) * HALF:(7 + c) * HALF],
                    scalar=gcol(("kvn", j), c), in1=t.rstd[:, 0:HALF], op0=ALU.mult, op1=ALU.mult),
                    reads=t.bigb + [t.rstdb, gpackb], writes=t.bigb)
            P.op("dve", lambda v: v.tensor_copy(out=latb[0:64, BO + 8 * HALF:BO + 9 * HALF],
                                                in_=latf[0:64, 8 * HALF:9 * HALF]),
                 reads=t.bigb, writes=t.bigb)
            for qq in range(2):
                qi = tile * 4 + h * 2 + qq
                P.dma("sp", LATS[qi].ap()[0:1024, :].rearrange("(c p) n -> p c n", p=128),
                      latb[:, BO:BO + 8 * HALF].rearrange("p (c n) -> p c n", n=HALF)[:, :, qq * 256:(qq + 1) * 256],
                      reads=t.bigb, writes=[LATSb[qi // 2]])
                P.dma("sp", LATS[qi].ap()[1024:LATR, :],
                      latb[0:64, BO + 8 * HALF + qq * 256:BO + 8 * HALF + (qq + 1) * 256],
                      reads=t.bigb, writes=[LATSb[qi // 2]])


    def mla_out(t, l, tile):
        j = l // 2
        wv = t.wst[:, 0:KC * 1024].rearrange("p (k c) -> p k c", c=1024)
        P.dma("sp", wv, wb[("wout", j)].ap().rearrange("(k p) c -> p k c", p=128),
              reads=[wbuf[("wout", j)]], writes=[t.wstb])
        for h in range(2):
            c0 = h * HALF
            for sl_ in range(4):
                cand = t.big[:, sl_ * 4096:(sl_ + 1) * 4096].rearrange("p (c n) -> p c n", n=HALF)
                for hh_ in range(2):
                    P.dma("sp", t.big[:, sl_ * 4096:(sl_ + 1) * 4096].rearrange("p (r h n) -> p r h n", h=2, n=HALF)[:, :, hh_, :],
                          OALL[hh_][sl_].ap()[:, tile * TT + c0:tile * TT + c0 + HALF].rearrange(
                              "(r p) n -> p r n", p=128),
                          reads=[OALLb[hh_][sl_]], writes=[t.candb[sl_][hh_]] + t.bigb)
            for sl_ in range(4):
                cand = t.big[:, sl_ * 4096:(sl_ + 1) * 4096].rearrange("p (c n) -> p c n", n=HALF)
                if sl_ == 0:
                    P.op("dve", lambda v, cand=cand: v.tensor_scalar(out=t.hT[:, :, c0:c0 + HALF], in0=cand,
                                                                     scalar1=pc[:, 129:130], scalar2=None,
                                                                     op0=ALU.mult),
                         reads=t.candb[sl_] + [pcb], writes=[t.hTb[h]])
                else:
                    P.op("dve", lambda v, cand=cand, sl_=sl_: v.scalar_tensor_tensor(
                        out=t.hT[:, :, c0:c0 + HALF], in0=cand, scalar=pc[:, 129 + sl_:130 + sl_],
                        in1=t.hT[:, :, c0:c0 + HALF], op0=ALU.mult, op1=ALU.add),
                        reads=t.candb[sl_] + [pcb, t.hTb[h]], writes=[t.hTb[h]])
        bi = 0
        for dc in range(KC):
            for h in range(2):
                c0, c1 = h * HALF, (h + 1) * HALF
                b = 4 + (bi % 2)
                bi += 1
                P.mm(ps[b][:], [(wv[:, k, dc * 128:(dc + 1) * 128], t.hT[:, k, c0:c1]) for k in range(KC)],
                     reads=[t.wstb, t.hTb[h]], writes=[psb[b]])
                P.op("dve", lambda v, b=b, dc=dc, c0=c0, c1=c1: v.tensor_tensor(
                    out=t.xT[:, dc, c0:c1], in0=ps[b][:], in1=t.xT[:, dc, c0:c1], op=ALU.add),
                    reads=[psb[b], t.xTb[dc]], writes=[t.xTb[dc]])

    def attention(l):
        j = l // 2
        st = ExitStack()

        def a(name, shape, dt):
            return st.enter_context(nc.sbuf_tensor(uname(name), list(shape), dt))

        Kn = a("Kn", [128, S], BF16)
        Kp = a("Kp", [128, S], BF16)
        V = a("V", [128, S], BF16)
        Knb = [Buf() for _ in range(NQT)]
        Kpb = [Buf() for _ in range(NQT)]
        Vb = [Buf() for _ in range(NQT)]
        cq = [a("cq%d" % i, [128, 6, HALF], BF16) for i in range(2)]
        ckv = [a("ckv%d" % i, [128, 2, HALF], BF16) for i in range(2)]
        kpe = [a("kpe%d" % i, [128, HALF], BF16) for i in range(2)]
        rc = [a("rc%d" % i, [64, HALF], F32) for i in range(2)]
        rs = [a("rs%d" % i, [64, HALF], F32) for i in range(2)]
        ldb = [Buf() for _ in range(2)]
        cqb = [[Buf(), Buf()] for _ in range(2)]
        ckvb = [[Buf(), Buf()] for _ in range(2)]
        kpeb = [[Buf(), Buf()] for _ in range(2)]
        rcb = [Buf() for _ in range(2)]
        rsb = [Buf() for _ in range(2)]
        wq = a("wq_sb", [128, 6, 448], BF16)
        wkv = a("wkv_sb", [128, 2, 512], BF16)
        wqb, wkvb = Buf(), Buf()
        Qn = [a("Qn%d" % i, [128, HALF], BF16) for i in range(2)]
        Qp = [a("Qp%d" % i, [128, HALF], BF16) for i in range(2)]
        Qnb = [Buf() for _ in range(2)]
        Qpb = [Buf() for _ in range(2)]
        pT = [a("pT%d" % i, [128, HALF], BF16) for i in range(4)]
        pTb = [Buf() for _ in range(4)]
        f1 = a("af1", [128, HALF], F32)
        f2 = a("af2", [128, HALF], F32)
        f3 = a("af3", [128, HALF], F32)
        f4 = a("af4", [128, HALF], F32)
        f1b, f2b, f3b, f4b = Buf(), Buf(), Buf(), Buf()
        sqn = a("asqn", [128, HALF], BF16)
        sqp = a("asqp", [128, HALF], BF16)
        sqnb, sqpb = Buf(), Buf()
        rst = a("arst", [128, HALF], F32)
        rstb = Buf()
        oT = [a("aoT%d" % i, [128, HALF], BF16) for i in range(2)]
        oTb = [Buf() for _ in range(2)]
        rl = a("arl", [128, HALF], F32)
        rlb = Buf()
        allb = (Knb + Kpb + Vb + [wqb, wkvb, f1b, f2b, f3b, f4b, sqnb, sqpb, rstb, rlb] + ldb + cqb[0] + cqb[1] + ckvb[0] + ckvb[1] + kpeb[0] + kpeb[1] +
                rcb + rsb + Qnb + Qpb + pTb + oTb)

        P.op("dve", lambda v: v.memset(Kp[:, :], 0.0), writes=Kpb)
        for i in range(2):
            P.op("dve", lambda v, i=i: v.memset(Qp[i][:, :], 0.0), writes=[Qpb[i]])
        P.op("dve", lambda v: v.memset(sqp[:, :], 0.0), writes=[sqpb])
        P.op("dve", lambda v: v.memset(f3[:, :], 0.0), writes=[f3b])
        P.op("dve", lambda v: v.memset(wq[:, :, :], 0.0), writes=[wqb])
        P.dma("sp", wq[:, :, 0:384], wb[("wq", j)].ap().rearrange("(k p) c -> p k c", p=128),
              reads=[wbuf[("wq", j)]], writes=[wqb])
        P.dma("sp", wkv[:, :, :], wb[("wkv", j)].ap().rearrange("(k p) c -> p k c", p=128),
              reads=[wbuf[("wkv", j)]], writes=[wkvb])

        PB = 7

        def load_tile(tq, i):
            rk = tq // 8
            base = rk * LATR
            for qq in range(2):
                qi = (tq % 8) * 2 + qq
                cs_ = slice(qq * 256, (qq + 1) * 256)
                LA = LATALL[qi].ap()
                P.dma("sp", cq[i][:, :, cs_], LA[base:base + QL, :].rearrange("(c p) n -> p c n", p=128),
                      reads=[LATALLb[qi // 2]], writes=[cqb[i][qq]])
                P.dma("sp", ckv[i][:, :, cs_],
                      LA[base + QL:base + QL + KVL, :].rearrange("(c p) n -> p c n", p=128),
                      reads=[LATALLb[qi // 2]], writes=[ckvb[i][qq]])
                P.dma("sp", kpe[i][0:64, cs_], LA[base + 1024:base + LATR, :],
                      reads=[LATALLb[qi // 2]], writes=[kpeb[i][qq]])
            P.dma("sp", rc[i][:, :], ROPEC.ap()[:, tq * HALF:(tq + 1) * HALF], reads=[ROPEb], writes=[rcb[i]])
            P.dma("sp", rs[i][:, :], ROPES.ap()[:, tq * HALF:(tq + 1) * HALF], reads=[ROPEb], writes=[rsb[i]])

        def proj_steps(hh, tq, i):
            steps = []
            ks = slice(tq * HALF, (tq + 1) * HALF)
            qn_col = gcol(("qhn_n", j))
            qp_col = gcol(("qhn_p", j))
            kn_col = gcol(("khn_n", j))
            kp_col = gcol(("khn_p", j))

            def head_norm(nope_ps_to_f, pe_src_fn, pe_srcbufs, gn, gp, dst_n, dst_nb, dst_p, dst_pb):
                def s1():
                    P.op("act", lambda a_: a_.activation(out=sqn[:, :], in_=f1[:, :], func=AF.Square),
                         reads=[f1b], writes=[sqnb])
                    P.op("act", lambda a_: a_.activation(out=sqp[0:64, :], in_=f2[0:64, :], func=AF.Square),
                         reads=[f2b], writes=[sqpb])
                    P.mm(ps[PB][:], [(ones[:], sqn[:, :]), (ones[:], sqp[:, :])],
                         reads=[sqnb, sqpb, onesb], writes=[psb[PB]])
                    P.op("act", lambda a_: a_.activation(out=f4[:, :], in_=ps[PB][:], func=AF.Sqrt,
                                                         bias=eps_t[:, 0:1], scale=1.0 / 192.0),
                         reads=[psb[PB]], writes=[f4b])
                    P.op("dve", lambda v: v.reciprocal(out=rst[:, :], in_=f4[:, :]), reads=[f4b], writes=[rstb])
                steps.append(s1)

                def s2():
                    P.op("dve", lambda v: v.scalar_tensor_tensor(out=dst_n, in0=f1[:, :], scalar=gn, in1=rst[:, :],
                                                                 op0=ALU.mult, op1=ALU.mult),
                         reads=[f1b, rstb, gpackb], writes=[dst_nb])
                    P.op("dve", lambda v: v.scalar_tensor_tensor(out=f3[0:64, :], in0=f2[0:64, :], scalar=gp[0:64, :],
                                                                 in1=rst[0:64, :], op0=ALU.mult, op1=ALU.mult),
                         reads=[f2b, rstb, gpackb], writes=[f3b])
                    P.mm(ps[PB][:], [(rotm[:, :], f3[:, :])], reads=[rotmb, f3b], writes=[psb[PB]])
                steps.append(s2)

                def s3():
                    P.op("dve", lambda v: v.tensor_tensor(out=f4[0:64, :], in0=ps[PB][0:64, :], in1=rs[i][:, :],
                                                          op=ALU.mult),
                         reads=[psb[PB], rsb[i]], writes=[f4b])
                    P.op("dve", lambda v: v.tensor_tensor(out=f2[0:64, :], in0=f3[0:64, :], in1=rc[i][:, :],
                                                          op=ALU.mult),
                         reads=[f3b, rcb[i]], writes=[f2b])
                    P.op("dve", lambda v: v.tensor_tensor(out=dst_p, in0=f2[0:64, :], in1=f4[0:64, :], op=ALU.add),
                         reads=[f2b, f4b], writes=[dst_pb])
                steps.append(s3)

            def k0():
                P.mm(ps[PB][:], [(wkv[:, c, hh * 256:hh * 256 + 128], ckv[i][:, c, :]) for c in range(2)],
                     reads=[wkvb] + ckvb[i], writes=[psb[PB]])
                P.op("act", lambda a_: a_.activation(out=f1[:, :], in_=ps[PB][:], func=AF.Copy),
                     reads=[psb[PB]], writes=[f1b])
                P.op("dve", lambda v: v.tensor_copy(out=f2[0:64, :], in_=kpe[i][0:64, :]),
                     reads=kpeb[i], writes=[f2b])
            steps.append(k0)
            head_norm(None, None, None, kn_col, kp_col, Kn[:, ks], Knb[tq], Kp[0:64, ks], Kpb[tq])

            def v0():
                for blk in range(4):
                    P.mm(ps[PB][:, blk * 128:(blk + 1) * 128],
                         [(ckv[i][:, c, blk * 128:(blk + 1) * 128], wkv[:, c, hh * 256 + 128:hh * 256 + 256])
                          for c in range(2)],
                         reads=[wkvb] + ckvb[i], writes=[psb[PB]], signal=(blk == 3))
                P.op("act", lambda a_: a_.activation(out=V[:, ks], in_=ps[PB][:], func=AF.Copy),
                     reads=[psb[PB]], writes=[Vb[tq]])
            steps.append(v0)

            def q0():
                P.mm(ps[PB][:], [(wq[:, c, hh * 192:hh * 192 + 128], cq[i][:, c, :]) for c in range(6)],
                     reads=[wqb] + cqb[i], writes=[psb[PB]])
                P.op("act", lambda a_: a_.activation(out=f1[:, :], in_=ps[PB][:], func=AF.Copy),
                     reads=[psb[PB]], writes=[f1b])
                P.mm(ps[PB][:], [(wq[:, c, hh * 192 + 128:hh * 192 + 256], cq[i][:, c, :]) for c in range(6)],
                     reads=[wqb] + cqb[i], writes=[psb[PB]])
                P.op("act", lambda a_: a_.activation(out=f2[0:64, :], in_=ps[PB][0:64, :], func=AF.Copy),
                     reads=[psb[PB]], writes=[f2b])
            steps.append(q0)
            head_norm(None, None, None, qn_col, qp_col, Qn[i][:, :], Qnb[i], Qp[i][0:64, :], Qpb[i])
            return steps

        def attn_tile(hh, tq, i, extra_steps):
            nkb = 4 * (tq + 1)
            ob = 3 + (tq % 2)
            lb = 5 + (tq % 2)
            nsteps = len(extra_steps)
            every = max(1, nkb // (nsteps + 1)) if nsteps else 0
            si = 0

            def s_mm(kb):
                b = kb % 3
                ksl = slice(kb * 128, (kb + 1) * 128)
                P.mm(ps[b][:], [(Kn[:, ksl], Qn[i][:, :]), (Kp[:, ksl], Qp[i][:, :])],
                     reads=[Knb[kb // 4], Kpb[kb // 4], Qnb[i], Qpb[i]], writes=[psb[b]])
                pi_ = kb % 4
                P.op("act", lambda a_: a_.activation(out=pT[pi_][:, :], in_=ps[b][:], func=AF.Exp,
                                                     bias=negc[:, j:j + 1], scale=SCALE),
                     reads=[psb[b], negcb], writes=[pTb[pi_]])
                d = kb - 4 * tq
                if d >= 0:
                    P.op("dve", lambda v: v.tensor_tensor(out=pT[pi_][:, :], in0=pT[pi_][:, :],
                                                          in1=masks[:, d * HALF:(d + 1) * HALF], op=ALU.mult),
                         reads=[pTb[pi_], masksb], writes=[pTb[pi_]])

            def pv_mm(kb):
                pi_ = kb % 4
                ksl = slice(kb * 128, (kb + 1) * 128)
                P.mm(ps[ob][:], [(V[:, ksl], pT[pi_][:, :])], reads=[Vb[kb // 4], pTb[pi_]], writes=[psb[ob]],
                     start=(kb == 0), stop=(kb == nkb - 1), signal=False)
                P.mm(ps[lb][:], [(ones[:, :], pT[pi_][:, :])], reads=[pTb[pi_], onesb, Vb[kb // 4]], writes=[psb[lb], psb[ob]],
                     start=(kb == 0), stop=(kb == nkb - 1), signal=True)

            oi = tq % 2
            if "noinner" in dbg:
                while si < nsteps:
                    extra_steps[si]()
                    si += 1
                P.op("dve", lambda v: v.memset(oT[oi][:, :], 0.0), writes=[oTb[oi]])
            else:
                for kb in range(nkb):
                    s_mm(kb)
                    if kb >= 2:
                        pv_mm(kb - 2)
                    if nsteps and si < nsteps and (kb % every == every - 1):
                        extra_steps[si]()
                        si += 1
                for kb in range(max(0, nkb - 2), nkb):
                    pv_mm(kb)
                while si < nsteps:
                    extra_steps[si]()
                    si += 1
                P.op("dve", lambda v: v.reciprocal(out=rl[:, :], in_=ps[lb][:]), reads=[psb[lb]], writes=[rlb])
                P.op("dve", lambda v: v.tensor_tensor(out=oT[oi][:, :], in0=ps[ob][:], in1=rl[:, :], op=ALU.mult),
                     reads=[psb[ob], rlb], writes=[oTb[oi]])
            g_ = tq // 8
            P.dma("sp", OSEND[hh][g_].ap()[:, (tq % 8) * HALF:(tq % 8 + 1) * HALF], oT[oi][:, :],
                  reads=[oTb[oi]], writes=[OSENDb[hh][g_]])
            if tq % 8 == 7:
                P.allgather(OSEND[hh][g_].ap(), OALL[hh][g_].ap(), groups4, reads=[OSENDb[hh][g_]],
                            writes=[OALLb[hh][g_]], fam="o")

        seq = [(hh, tq) for hh in range(2) for tq in range(NQT)]
        load_tile(seq[0][1], 0)
        for s_ in proj_steps(seq[0][0], seq[0][1], 0):
            s_()
        for n, (hh, tq) in enumerate(seq):
            i = n % 2
            extra = []
            if n + 1 < len(seq):
                nh, nt = seq[n + 1]
                load_tile(nt, 1 - i)
                extra = proj_steps(nh, nt, 1 - i)
            attn_tile(hh, tq, i, extra)
        P.barrier(barscr[:, 0:1])
        P.release(allb)
        st.close()

    def setup_rope_and_negc():
        st = ExitStack()

        def a(name, shape, dt):
            return st.enter_context(nc.sbuf_tensor(uname(name), list(shape), dt))

        posi = a("posi", [64, HALF], I32)
        posib = Buf()
        t1 = a("rp1", [64, HALF], F32)
        t2 = a("rp2", [64, HALF], F32)
        t3 = a("rp3", [64, HALF], F32)
        t4 = a("rp4", [64, HALF], F32)
        so = [a("rpso%d" % i, [64, HALF], F32) for i in range(2)]
        co = [a("rpco%d" % i, [64, HALF], F32) for i in range(2)]
        t1b, t2b, t3b, t4b = Buf(), Buf(), Buf(), Buf()
        sob = [Buf() for _ in range(2)]
        cob = [Buf() for _ in range(2)]
        hn = a("hnrow", [1, 4 * 192], F32)
        hnb = Buf()
        mx = a("hnmx", [1, 8], F32)
        mxb = Buf()
        onesf = a("onesf", [1, 128], F32)
        onesfb = Buf()
        invf = gcol("invf")
        for tq in range(NQT):
            i = tq % 2
            sl = slice(tq * HALF, (tq + 1) * HALF)
            P.dma("sp", posi[:, :], pos_in.ap()[0:1, sl].partition_broadcast(64), writes=[posib])
            P.op("dve", lambda v: v.tensor_copy(out=t1[:, :], in_=posi[:, :]), reads=[posib], writes=[t1b])
            P.op("dve", lambda v: v.tensor_scalar(out=t1[:, :], in0=t1[:, :], scalar1=invf[0:64, :], scalar2=None,
                                                  op0=ALU.mult),
                 reads=[t1b, gpackb], writes=[t1b])
            P.op("dve", lambda v: v.tensor_scalar(out=t2[:, :], in0=t1[:, :], scalar1=1.0 / TWO_PI, scalar2=MAGIC,
                                                  op0=ALU.mult, op1=ALU.add), reads=[t1b], writes=[t2b])
            P.op("dve", lambda v: v.tensor_scalar(out=t2[:, :], in0=t2[:, :], scalar1=-MAGIC, scalar2=None,
                                                  op0=ALU.add), reads=[t2b], writes=[t2b])
            for cc in (C1, C2, C3):
                P.op("dve", lambda v, cc=cc: v.scalar_tensor_tensor(out=t1[:, :], in0=t2[:, :], scalar=-cc,
                                                                    in1=t1[:, :], op0=ALU.mult, op1=ALU.add),
                     reads=[t2b, t1b], writes=[t1b])
            P.op("dve", lambda v: v.tensor_scalar(out=t3[:, :], in0=t1[:, :], scalar1=PI_SAFE, scalar2=-PI_SAFE,
                                                  op0=ALU.min, op1=ALU.max), reads=[t1b], writes=[t3b])
            P.op("act", lambda a_: a_.activation(out=so[i][:, :], in_=t3[:, :], func=AF.Sin),
                 reads=[t3b], writes=[sob[i]])
            P.op("dve", lambda v: v.tensor_scalar(out=t3[:, :], in0=t1[:, :], scalar1=math.pi / 2, scalar2=None,
                                                  op0=ALU.add), reads=[t1b, sob[i]], writes=[t3b])
            P.op("dve", lambda v: v.tensor_scalar(out=t4[:, :], in0=t3[:, :], scalar1=math.pi, scalar2=-TWO_PI,
                                                  op0=ALU.is_gt, op1=ALU.mult), reads=[t3b], writes=[t4b])
            P.op("dve", lambda v: v.tensor_tensor(out=t3[:, :], in0=t3[:, :], in1=t4[:, :], op=ALU.add),
                 reads=[t3b, t4b], writes=[t3b])
            P.op("dve", lambda v: v.tensor_scalar(out=t3[:, :], in0=t3[:, :], scalar1=PI_SAFE, scalar2=-PI_SAFE,
                                                  op0=ALU.min, op1=ALU.max), reads=[t3b], writes=[t3b])
            P.op("act", lambda a_: a_.activation(out=co[i][:, :], in_=t3[:, :], func=AF.Sin),
                 reads=[t3b], writes=[cob[i]])
            P.dma("sp", ROPES.ap()[:, sl], so[i][:, :], reads=[sob[i]], writes=[ROPEb])
            P.dma("sp", ROPEC.ap()[:, sl], co[i][:, :], reads=[cob[i]], writes=[ROPEb])
        P.dma("sp", hn[:, :], hn_in.ap(), writes=[hnb])
        P.op("dve", lambda v: v.memset(onesf[:, :], 1.0), writes=[onesfb])
        for q in range(4):
            P.op("dve", lambda v, q=q: v.tensor_reduce(out=mx[:, q:q + 1], in_=hn[:, q * 192:(q + 1) * 192],
                                                       axis=mybir.AxisListType.X, op=ALU.max,
                                                       apply_absolute_value=True),
                 reads=[hnb], writes=[mxb])
        for jj in range(2):
            P.op("dve", lambda v, jj=jj: v.tensor_tensor(out=mx[:, 4 + jj:5 + jj], in0=mx[:, 2 * jj:2 * jj + 1],
                                                         in1=mx[:, 2 * jj + 1:2 * jj + 2], op=ALU.mult),
                 reads=[mxb], writes=[mxb])
        P.mm(ps[7][:, 0:2], [(onesf[:, :], mx[:, 4:6])], reads=[onesfb, mxb], writes=[psb[7]])
        P.op("dve", lambda v: v.tensor_scalar(out=negc[:, :], in0=ps[7][:, 0:2], scalar1=-math.sqrt(192.0),
                                              scalar2=None, op0=ALU.mult), reads=[psb[7]], writes=[negcb])
        P.barrier(barscr[:, 0:1])
        P.release([posib, t1b, t2b, t3b, t4b, hnb, mxb, onesfb] + sob + cob)
        st.close()

    if "nocast" not in dbg:
        for fam in ("f1g", "f1u", "f1d", "poolw", "f2g", "f2u", "f2d", "win", "wout"):
            cast_family(fam)
        cast_small()
    if "norope" not in dbg:
        setup_rope_and_negc()

    TORDER = [3, 0, 1, 2]

    def token_phase(body):
        st = ExitStack()
        t = alloc_tok(st)
        body(t)
        P.barrier(barscr[:, 0:1])
        P.release(t.allb)
        st.close()

    inb = [Buf() for _ in range(NT)]
    outb = [Buf() for _ in range(NT)]

    def final_store(t, tile):
        store_x(t, outT, [outb[tile]], tile)

    def seg_A(t):
        for tile in TORDER:
            load_x(t, xT_in, [], tile)
            if "noffn" not in dbg:
                ffn(t, "f1", 0)
            if tile == NT - 1 and "nohalo" not in dbg:
                halo_tail_to_send(t, 0)
                halo_exchange()
            store_x(t, XP, [XPb[tile]], tile)

    def seg_B(l):
        def body(t):
            for tile in range(NT):
                load_x(t, XP, [XPb[tile]], tile)
                if "nopool" not in dbg:
                    pool_mixer(t, l, tile)
                if "noffn" not in dbg:
                    ffn(t, "f2", l)
                    ffn(t, "f1", l + 1)
                if "nolat" not in dbg:
                    mla_latents(t, l + 1, tile)
                store_x(t, XP, [XPb[tile]], tile)
            if "nolat" not in dbg and "noag" not in dbg:
                for qi in range(NQ4):
                    P.allgather(LATS[qi].ap(), LATALL[qi].ap(), groups4, reads=[LATSb[qi // 2]],
                                writes=[LATALLb[qi // 2]], fam="l")

        return body

    def seg_C(l, last):
        def body(t):
            order = list(range(NT)) if last else TORDER
            for tile in order:
                load_x(t, XP, [XPb[tile]], tile)
                mla_out(t, l, tile)
                ffn(t, "f2", l)
                if last:
                    final_store(t, tile)
                else:
                    ffn(t, "f1", l + 1)
                    if tile == NT - 1:
                        halo_tail_to_send(t, l + 1)
                        halo_exchange()
                    store_x(t, XP, [XPb[tile]], tile)
        return body

    def dump_and_finish(t):
        for tile in range(NT):
            load_x(t, XP, [XPb[tile]], tile)
            final_store(t, tile)

    token_phase(seg_A)
    if upto >= 2:
        token_phase(seg_B(0))
    if upto >= 3:
        attention(1)
    if upto >= 4:
        token_phase(seg_C(1, False))
    if upto >= 5:
        token_phase(seg_B(2))
    if upto >= 6:
        attention(3)
    if upto >= 7:
        token_phase(seg_C(3, True))
    else:
        token_phase(dump_and_finish)
    P.barrier(barscr[:, 0:1])
    return nc


def _col(v):
    v = np.asarray(v, np.float32)
    n = v.shape[0] // 128
    return np.ascontiguousarray(v.reshape(n, 128).T)


def make_in_maps(inp):
    f32 = np.float32
    x = np.asarray(inp["x"], f32)
    pos = np.asarray(inp["positions"], np.int32)
    gp = np.zeros((128, NG), f32)
    for l in range(4):
        gp[:, GP[("f1n", l)]:GP[("f1n", l)] + 8] = _col(inp["ffn1_norm"][l])
        gp[:, GP[("mixn", l)]:GP[("mixn", l)] + 8] = _col(inp["mix_norm"][l])
        gp[:, GP[("f2n", l)]:GP[("f2n", l)] + 8] = _col(inp["ffn2_norm"][l])
    for j in range(2):
        gp[:, GP[("pscale", j)]:GP[("pscale", j)] + 8] = _col(inp["pool_scale"][j])
        gp[:, GP[("qn", j)]:GP[("qn", j)] + 6] = _col(inp["mla_q_norm"][j])
        gp[:, GP[("kvn", j)]:GP[("kvn", j)] + 2] = _col(inp["mla_kv_norm"][j])
        qh = np.asarray(inp["mla_q_head_norm"][j], f32)
        kh = np.asarray(inp["mla_k_head_norm"][j], f32)
        gp[:, GP[("qhn_n", j)]] = qh[0:128]
        gp[0:64, GP[("qhn_p", j)]] = qh[128:192]
        gp[:, GP[("khn_n", j)]] = kh[0:128]
        gp[0:64, GP[("khn_p", j)]] = kh[128:192]
    invf = (1.0 / (np.float32(10000.0) ** (np.arange(0, 64, 2, dtype=f32) / np.float32(64)))).astype(f32)
    gp[0:32, GP["invf"]] = invf
    gp[32:64, GP["invf"]] = invf
    hn = np.concatenate([np.asarray(inp["mla_q_head_norm"][0], f32), np.asarray(inp["mla_k_head_norm"][0], f32),
                         np.asarray(inp["mla_q_head_norm"][1], f32), np.asarray(inp["mla_k_head_norm"][1], f32)])[None, :]
    kk = np.arange(128)[:, None]
    qq = np.arange(HALF)[None, :]
    masks = np.concatenate([(kk + 128 * d <= qq) for d in range(4)], axis=1).astype(f32).astype(ml_dtypes.bfloat16)
    rotm = np.zeros((128, 128), f32)
    for m in range(32):
        rotm[m + 32, m] = -1.0
        rotm[m, m + 32] = 1.0
    shared = {
        "gpack": gp, "hn": np.ascontiguousarray(hn), "masks": np.ascontiguousarray(masks), "rotm": rotm,
    }
    fams = {
        "poolw": np.asarray(inp["pool_w"], f32).reshape(2 * 1024, 256),
        "win": np.asarray(inp["mla_w_in"], f32).reshape(2 * D, LATR),
        "wout": np.asarray(inp["mla_w_out"], f32).reshape(2 * D, D),
    }
    for f, nm in (("f1", "ffn1"), ("f2", "ffn2")):
        fams[f + "g"] = np.asarray(inp[nm + "_w_gate"], f32).reshape(4 * D, DFF)
        fams[f + "u"] = np.asarray(inp[nm + "_w_up"], f32).reshape(4 * D, DFF)
        fams[f + "d"] = np.asarray(inp[nm + "_w_down"], f32).reshape(4 * DFF, D)
    wq = np.asarray(inp["mla_w_q_up"], f32)
    wkv = np.asarray(inp["mla_w_kv_up"], f32)
    maps = []
    for c in range(NCORES):
        b, r = c // 4, c % 4
        m = dict(shared)
        for fam, arr in fams.items():
            rr = arr.shape[0] // NCORES
            m[fam] = np.ascontiguousarray(arr[c * rr:(c + 1) * rr])
        m["xT"] = np.ascontiguousarray(x[b, r * TOK:(r + 1) * TOK, :].T)
        m["pos"] = np.ascontiguousarray(pos[b][None, :])
        m["wqm"] = np.ascontiguousarray(wq[:, :, r * 384:(r + 1) * 384].reshape(2 * QL, 384))
        m["wkvm"] = np.ascontiguousarray(wkv[:, :, r * 512:(r + 1) * 512].reshape(2 * KVL, 512))
        ic = np.zeros((128, 137), f32)
        ic[:, 129 + r] = 1.0
        if r > 0:
            ic[:, 133 + (r - 1)] = 1.0
        for ch in range(8):
            w = 2 << (ch // 2)
            tg = r * TOK + np.arange(16)
            ic[:, ch * 16:(ch + 1) * 16] = (1.0 / np.minimum(tg + 1, w)).astype(f32)[None, :]
        m["invcnt"] = ic
        maps.append(m)
    return maps


_NC_CACHE = {}


def kernel(**inputs):
    upto = 99
    if upto not in _NC_CACHE:
        _NC_CACHE[upto] = build_program(upto)
    nc = _NC_CACHE[upto]
    maps = make_in_maps(inputs)
    res = run_bass_kernel_spmd(nc, maps, core_ids=list(range(NCORES)))
    out = np.empty((2, S, D), np.float32)
    for c in range(NCORES):
        b, r = c // 4, c % 4
        out[b, r * TOK:(r + 1) * TOK, :] = res.results[c]["outT"].T
    return out
```
